# Optimizing a Trainium2 kernel written in Bass

```python
import math
import jax
import jax.numpy as jnp
from jax import lax
import numpy as np


D_MODEL = 2048
BATCH = 2
SEQ = 4096
DEPTH = 2
DEC_BATCH = 1
DEC_SEQ = 16384
PAST_LEN = 128

HEAD_DIM = 128
A_HEADS = 8
DILATED_CFGS = ((128, 1), (512, 4), (2048, 16))
SWA_BLOCK = 64
B_GROUPS = 8
B_GROUP_DIM = 128
CHUNK = 128
C_HEADS = 12
C_HALF = 64
DIFF_Q_BLOCK = 128
D_GROUPS = 4
D_GROUP_DIM = 128
A_W = A_HEADS * HEAD_DIM
B_W = B_GROUPS * B_GROUP_DIM
C_QK_W = C_HEADS * 2 * C_HALF
C_V_W = C_HEADS * 2 * C_HALF
D_W = D_GROUPS * D_GROUP_DIM
MIX_IN = 3 * A_W + 2 * B_W
MIX_OUT = A_W + B_W
ROPE_THETA = 500000.0
ROPE_FRACTION = 4
MEM_LEN = 256
CA_HEADS = 4
CA_HEAD_DIM = 128
CA_W = CA_HEADS * CA_HEAD_DIM
D_FF = 5632
CONV_W = 3
EPS = 1e-6
N_EVEN = (DEPTH + 1) // 2
N_ODD = DEPTH // 2

kernel_name = 'hybrid_dilated_gmlp_diffattn_fnet_encoder'


def _rmsnorm(x, g):
    xf = x.astype(jnp.float32)
    y = xf * lax.rsqrt(jnp.mean(xf * xf, axis=-1, keepdims=True) + EPS)
    return (y * g.astype(jnp.float32)).astype(x.dtype)


def _rope_partial(x, pos):
    rd = x.shape[-1] // ROPE_FRACTION
    half = rd // 2
    inv = ROPE_THETA ** (-jnp.arange(half, dtype=jnp.float32) / half)
    ang = pos.astype(jnp.float32)[:, None] * inv[None, :]
    cos = jnp.cos(ang)[:, None, :]
    sin = jnp.sin(ang)[:, None, :]
    x1 = x[..., :half].astype(jnp.float32)
    x2 = x[..., half:rd].astype(jnp.float32)
    rot = jnp.concatenate([x1 * cos - x2 * sin, x2 * cos + x1 * sin], axis=-1).astype(x.dtype)
    return jnp.concatenate([rot, x[..., rd:]], axis=-1)


def _dilated_window_attention(q, k, v):
    Bn, S, H, dh = q.shape
    scale = dh ** -0.5
    ms, nums, dens = [], [], []
    for window, dil in DILATED_CFGS:
        radius = window // (2 * dil)
        L = S // dil
        nblk = -(-L // SWA_BLOCK)
        Lp = nblk * SWA_BLOCK

        def to_sub(t):
            t = t.reshape(Bn, L, dil, H, dh)
            t = jnp.pad(t, ((0, 0), (0, Lp - L), (0, 0), (0, 0), (0, 0)))
            return t.reshape(Bn, nblk, SWA_BLOCK, dil, H, dh)

        def neigh(t):
            z = jnp.zeros_like(t[:, :1])
            prev = jnp.concatenate([z, t[:, :-1]], axis=1)
            nxt = jnp.concatenate([t[:, 1:], z], axis=1)
            return jnp.concatenate([prev, t, nxt], axis=2)

        qs = to_sub(q)
        kn = neigh(to_sub(k))
        vn = neigh(to_sub(v))
        qpos = jnp.arange(SWA_BLOCK)
        kpos = jnp.arange(3 * SWA_BLOCK) - SWA_BLOCK
        kabs = (jnp.arange(nblk) * SWA_BLOCK)[:, None] + kpos[None, :]
        band = jnp.abs(kpos[None, :] - qpos[:, None]) <= radius
        mask = band[None] & ((kabs >= 0) & (kabs < L))[:, None, :]
        s = jnp.einsum('bnqrhd,bnkrhd->bnrhqk', qs, kn,
                       preferred_element_type=jnp.float32) * scale
        s = jnp.where(mask[None, :, None, None], s, -jnp.inf)
        m = jnp.max(s, axis=-1)
        p = jnp.exp(s - m[..., None])
        l = jnp.sum(p, axis=-1)
        num = jnp.einsum('bnrhqk,bnkrhd->bnqrhd', p, vn.astype(jnp.float32))
        num = num.reshape(Bn, Lp, dil, H, dh)[:, :L].reshape(Bn, S, H, dh)
        m = m.transpose(0, 1, 4, 2, 3).reshape(Bn, Lp, dil, H)[:, :L].reshape(Bn, S, H)
        l = l.transpose(0, 1, 4, 2, 3).reshape(Bn, Lp, dil, H)[:, :L].reshape(Bn, S, H)
        ms.append(m)
        nums.append(num)
        dens.append(l)
    m_all = jnp.stack(ms)
    w = jnp.exp(m_all - jnp.max(m_all, axis=0, keepdims=True))
    den = jnp.sum(w * jnp.stack(dens), axis=0)
    num = jnp.sum(w[..., None] * jnp.stack(nums), axis=0)
    return (num / den[..., None]).astype(q.dtype)


def _chunked_spatial_gating(u, v, w_s, b_s, g_ln):
    Bn, S, G, c = v.shape
    vf = v.astype(jnp.float32)
    mu = jnp.mean(vf, axis=-1, keepdims=True)
    var = jnp.mean(jnp.square(vf - mu), axis=-1, keepdims=True)
    vn = (vf - mu) * lax.rsqrt(var + EPS) * g_ln.astype(jnp.float32)
    vc = vn.reshape(Bn, S // CHUNK, CHUNK, G, c)
    sv = jnp.einsum('gij,bnjgc->bnigc', w_s.astype(jnp.float32), vc) \
        + b_s.astype(jnp.float32).T[None, None, :, :, None]
    return (u.astype(jnp.float32) * sv.reshape(Bn, S, G, c)).astype(u.dtype)


def _diff_attention(q, k, v, lam, subln_g, layer_idx):
    Bn, S, H, _, dc = q.shape
    scale = dc ** -0.5
    lam_init = 0.8 - 0.6 * math.exp(-0.3 * layer_idx)
    lamf = lam.astype(jnp.float32)
    lam_full = jnp.exp(jnp.sum(lamf[0] * lamf[1])) - jnp.exp(jnp.sum(lamf[2] * lamf[3])) + lam_init
    nqb = S // DIFF_Q_BLOCK
    qb = q.reshape(Bn, nqb, DIFF_Q_BLOCK, H, 2, dc).transpose(1, 0, 2, 3, 4, 5)
    vf = v.astype(jnp.float32)

    def one_block(qblk):
        s = jnp.einsum('bqhtd,bkhtd->bthqk', qblk, k,
                       preferred_element_type=jnp.float32) * scale
        p = jax.nn.softmax(s, axis=-1)
        a = p[:, 0] - lam_full * p[:, 1]
        return jnp.einsum('bhqk,bkhd->bqhd', a, vf)

    o = lax.map(one_block, qb)
    o = o.transpose(1, 0, 2, 3, 4).reshape(Bn, S, H, 2 * dc)
    o = _rmsnorm(o, subln_g) * (1.0 - lam_init)
    return o.astype(v.dtype)


def _fourier_mix(z, w_f):
    f = jnp.fft.fft2(z.astype(jnp.float32), axes=(1, 3), norm='ortho').real
    return jnp.einsum('bsgc,gce->bsge', f, w_f.astype(jnp.float32)).astype(z.dtype)


def _even_mixer(proj, pos, w_s, b_s, g_ln):
    Bn, S, _ = proj.shape
    qa = _rope_partial(proj[..., 0:A_W].reshape(Bn, S, A_HEADS, HEAD_DIM), pos)
    ka = _rope_partial(proj[..., A_W:2 * A_W].reshape(Bn, S, A_HEADS, HEAD_DIM), pos)
    va = proj[..., 2 * A_W:3 * A_W].reshape(Bn, S, A_HEADS, HEAD_DIM)
    ub = jax.nn.gelu(proj[..., 3 * A_W:3 * A_W + B_W]).reshape(Bn, S, B_GROUPS, B_GROUP_DIM)
    vb = jax.nn.gelu(proj[..., 3 * A_W + B_W:]).reshape(Bn, S, B_GROUPS, B_GROUP_DIM)
    ya = _dilated_window_attention(qa, ka, va).reshape(Bn, S, A_W)
    yb = _chunked_spatial_gating(ub, vb, w_s, b_s, g_ln).reshape(Bn, S, B_W)
    return jnp.concatenate([ya, yb], axis=-1)


def _odd_mixer(proj, pos, lam, subln_g, w_f, layer_idx):
    Bn, S, _ = proj.shape
    qc = proj[..., 0:C_QK_W].reshape(Bn, S, C_HEADS * 2, C_HALF)
    kc = proj[..., C_QK_W:2 * C_QK_W].reshape(Bn, S, C_HEADS * 2, C_HALF)
    qc = _rope_partial(qc, pos).reshape(Bn, S, C_HEADS, 2, C_HALF)
    kc = _rope_partial(kc, pos).reshape(Bn, S, C_HEADS, 2, C_HALF)
    vc = proj[..., 2 * C_QK_W:2 * C_QK_W + C_V_W].reshape(Bn, S, C_HEADS, 2 * C_HALF)
    zd = proj[..., 2 * C_QK_W + C_V_W:].reshape(Bn, S, D_GROUPS, D_GROUP_DIM)
    yc = _diff_attention(qc, kc, vc, lam, subln_g, layer_idx).reshape(Bn, S, C_V_W)
    yd = _fourier_mix(zd, w_f).reshape(Bn, S, D_W)
    return jnp.concatenate([yc, yd], axis=-1)


def _memory_cross_attention(h, mem, g_mem, wq, wkv, wo):
    Bn, S, _ = h.shape
    M = mem.shape[1]
    m = _rmsnorm(mem, g_mem)
    q = (h @ wq).reshape(Bn, S, CA_HEADS, CA_HEAD_DIM)
    kv = (m @ wkv).reshape(Bn, M, 2, CA_HEADS, CA_HEAD_DIM)
    s = jnp.einsum('bqhd,bkhd->bhqk', q, kv[:, :, 0],
                   preferred_element_type=jnp.float32) * (CA_HEAD_DIM ** -0.5)
    p = jax.nn.softmax(s, axis=-1)
    o = jnp.einsum('bhqk,bkhd->bqhd', p, kv[:, :, 1].astype(jnp.float32)).astype(h.dtype)
    return o.reshape(Bn, S, CA_W) @ wo


def _conv_gated_ffn(h, w_up, conv_w, conv_b, w_down):
    S = h.shape[1]
    z = h @ w_up
    pad = CONV_W // 2
    zp = jnp.pad(z, ((0, 0), (pad, pad), (0, 0)))
    z = sum(zp[:, j:j + S] * conv_w[j] for j in range(CONV_W)) + conv_b
    gate = z[..., :D_FF]
    val = z[..., D_FF:]
    return (jax.nn.gelu(gate) * val) @ w_down


def _trunk(x, mem, norm_gains, w_in, w_out, b_w_spatial, b_b_spatial, b_ln_gain,
           c_lambda, c_subln_gain, d_w_fourier, mem_norm_gain, ca_w_q, ca_w_kv, ca_w_o,
           ffn_w_up, ffn_conv_w, ffn_conv_b, ffn_w_down):
    S = x.shape[1]
    pos = jnp.arange(S)
    for layer in range(DEPTH):
        g = norm_gains[layer]
        proj = _rmsnorm(x, g[0]) @ w_in[layer]
        if layer % 2 == 0:
            e = layer // 2
            mix = _even_mixer(proj, pos, b_w_spatial[e], b_b_spatial[e], b_ln_gain[e])
        else:
            o = layer // 2
            mix = _odd_mixer(proj, pos, c_lambda[o], c_subln_gain[o], d_w_fourier[o], layer)
        x = x + _rmsnorm(mix @ w_out[layer], g[1])
        ca = _memory_cross_attention(_rmsnorm(x, g[2]), mem, mem_norm_gain[layer],
                                     ca_w_q[layer], ca_w_kv[layer], ca_w_o[layer])
        x = x + _rmsnorm(ca, g[3])
        ff = _conv_gated_ffn(_rmsnorm(x, g[4]), ffn_w_up[layer], ffn_conv_w[layer],
                             ffn_conv_b[layer], ffn_w_down[layer])
        x = x + _rmsnorm(ff, g[5])
    return x


def setup_inputs(seed: int = 0) -> dict:
    key = jax.random.key(seed)
    ks = jax.random.split(key, 24)
    f32 = jnp.float32

    def nrm(k, shape, scale):
        return jax.random.normal(k, shape, f32) * scale

    return {
        'x_prompt': nrm(ks[0], (BATCH, SEQ, D_MODEL), 1.0),
        'x_sample': nrm(ks[1], (DEC_BATCH, DEC_SEQ, D_MODEL), 1.0),
        'mem_prompt': nrm(ks[2], (BATCH, MEM_LEN, D_MODEL), 1.0),
        'mem_sample': nrm(ks[3], (DEC_BATCH, MEM_LEN, D_MODEL), 1.0),
        'norm_gains': 1.0 + nrm(ks[4], (DEPTH, 6, D_MODEL), 0.02),
        'w_in': nrm(ks[5], (DEPTH, D_MODEL, MIX_IN), D_MODEL ** -0.5),
        'w_out': nrm(ks[6], (DEPTH, MIX_OUT, D_MODEL), MIX_OUT ** -0.5),
        'b_w_spatial': nrm(ks[7], (N_EVEN, B_GROUPS, CHUNK, CHUNK), CHUNK ** -0.5),
        'b_b_spatial': 1.0 + nrm(ks[8], (N_EVEN, B_GROUPS, CHUNK), 0.02),
        'b_ln_gain': 1.0 + nrm(ks[9], (N_EVEN, B_GROUPS, B_GROUP_DIM), 0.02),
        'c_lambda': nrm(ks[10], (N_ODD, 4, C_HALF), 0.1),
        'c_subln_gain': 1.0 + nrm(ks[11], (N_ODD, 2 * C_HALF), 0.02),
        'd_w_fourier': nrm(ks[12], (N_ODD, D_GROUPS, D_GROUP_DIM, D_GROUP_DIM), D_GROUP_DIM ** -0.5),
        'mem_norm_gain': 1.0 + nrm(ks[13], (DEPTH, D_MODEL), 0.02),
        'ca_w_q': nrm(ks[14], (DEPTH, D_MODEL, CA_W), D_MODEL ** -0.5),
        'ca_w_kv': nrm(ks[15], (DEPTH, D_MODEL, 2 * CA_W), D_MODEL ** -0.5),
        'ca_w_o': nrm(ks[16], (DEPTH, CA_W, D_MODEL), CA_W ** -0.5),
        'ffn_w_up': nrm(ks[17], (DEPTH, D_MODEL, 2 * D_FF), D_MODEL ** -0.5),
        'ffn_conv_w': nrm(ks[18], (DEPTH, CONV_W, 2 * D_FF), CONV_W ** -0.5),
        'ffn_conv_b': nrm(ks[19], (DEPTH, 2 * D_FF), 0.02),
        'ffn_w_down': nrm(ks[20], (DEPTH, D_FF, D_MODEL), D_FF ** -0.5),
    }


def reference(x_prompt, x_sample, mem_prompt, mem_sample, norm_gains, w_in, w_out,
              b_w_spatial, b_b_spatial, b_ln_gain, c_lambda, c_subln_gain, d_w_fourier,
              mem_norm_gain, ca_w_q, ca_w_kv, ca_w_o, ffn_w_up, ffn_conv_w, ffn_conv_b,
              ffn_w_down):
    y_prompt = _trunk(x_prompt, mem_prompt, norm_gains, w_in, w_out, b_w_spatial, b_b_spatial,
                      b_ln_gain, c_lambda, c_subln_gain, d_w_fourier, mem_norm_gain, ca_w_q,
                      ca_w_kv, ca_w_o, ffn_w_up, ffn_conv_w, ffn_conv_b, ffn_w_down)
    y_sample = _trunk(x_sample, mem_sample, norm_gains, w_in, w_out, b_w_spatial, b_b_spatial,
                      b_ln_gain, c_lambda, c_subln_gain, d_w_fourier, mem_norm_gain, ca_w_q,
                      ca_w_kv, ca_w_o, ffn_w_up, ffn_conv_w, ffn_conv_b, ffn_w_down)
    return (y_prompt, y_sample)
```

```python
import math
import numpy as np
from contextlib import ExitStack
import ml_dtypes
import concourse.bass as bass
import concourse.mybir as mybir
from concourse.bass_utils import run_bass_kernel_spmd

F32 = mybir.dt.float32
BF16 = mybir.dt.bfloat16
AF = mybir.ActivationFunctionType
ALU = mybir.AluOpType
AX = mybir.AxisListType

NCORES = 8
D = 2048
KC = 16
NTOK = 3072
NCH = 24
TG = 4
NG = NCH // TG
SEG_OFF = [0, 512, 1024]
SEG_LEN = [512, 512, 2048]
SEG_S = [4096, 4096, 16384]
EOFF = [0, 2560, 5120]
ETOT = 9216
MIX_IN = 5120
D_FF = 5632
EPS = 1e-6
GRP_SEG = [0, 1, 2, 2, 2, 2]


class StopPhase(Exception):
    pass


class Prog:
    def __init__(self, nc, st):
        self.nc = nc
        self.st = st
        self.eng = {'pe': nc.tensor, 'act': nc.scalar, 'dve': nc.vector, 'pool': nc.gpsimd, 'sp': nc.sync}
        self.sems = {}
        self.cnt = {}
        for k in self.eng:
            self.sems['es_' + k] = st.enter_context(nc.semaphore('es_' + k))
            self.cnt['es_' + k] = 0
        self.seen = {k: {} for k in self.eng}
        self.bw = {}
        self.br = {}
        self.const = set()
        self.ninst = 0

    def _wait(self, e, deps):
        need = {}
        seen = self.seen[e]
        for (name, val) in deps:
            if seen.get(name, 0) >= val:
                continue
            if need.get(name, 0) < val:
                need[name] = val
        for name, val in need.items():
            self.eng[e].wait_ge(self.sems[name], val)
            seen[name] = val
            self.ninst += 1

    def _deps(self, r, w):
        deps = []
        for k in r:
            ev = self.bw.get(k)
            if ev is not None:
                deps.append(ev)
        for k in w:
            ev = self.bw.get(k)
            if ev is not None:
                deps.append(ev)
            rd = self.br.get(k)
            if rd:
                deps.extend(rd.items())
        return deps

    def _commit(self, ev, r, w):
        for k in r:
            if k in self.const:
                continue
            d = self.br.setdefault(k, {})
            if d.get(ev[0], 0) < ev[1]:
                d[ev[0]] = ev[1]
        for k in w:
            self.bw[k] = ev
            self.br[k] = {}

    def op(self, e, fn, r=(), w=()):
        deps = self._deps(r, w)
        if e == 'pe':
            deps = [d for d in deps if d[0] != 'es_pe']
        self._wait(e, deps)
        inst = fn(self.eng[e])
        name = 'es_' + e
        self.cnt[name] += 1
        inst.then_inc(self.sems[name], 1)
        self.ninst += 1
        self._commit((name, self.cnt[name]), r, w)

    def dma(self, q, out, in_, r=(), w=(), skey=None):
        name = 'ds_' + skey
        if name not in self.sems:
            self.sems[name] = self.st.enter_context(self.nc.semaphore(name))
            self.cnt[name] = 0
        deps = self._deps(r, w)
        if self.cnt[name] > 0:
            deps.append((name, self.cnt[name]))
        self._wait(q, deps)
        inst = self.eng[q].dma_start(out=out, in_=in_)
        self.cnt[name] += 16
        inst.then_inc(self.sems[name], 16)
        self.ninst += 1
        self._commit((name, self.cnt[name]), r, w)

    def collective(self, kind, in_ap, out_ap, r=(), w=(), skey=None):
        name = 'cs_' + skey
        self.sems[name] = self.st.enter_context(self.nc.semaphore(name))
        self.cnt[name] = 0
        deps = self._deps(r, w)
        self._wait('pool', deps)
        inst = self.nc.gpsimd.collective_compute(kind, ALU.bypass, replica_groups=[list(range(NCORES))],
                                                 ins=[in_ap], outs=[out_ap])
        self.cnt[name] += 1
        inst.then_inc(self.sems[name], 1)
        self._commit((name, 1), r, w)

    def barrier(self):
        allev = [(n, c) for n, c in self.cnt.items() if c > 0]
        for e in self.eng:
            self._wait(e, allev)
        self.bw.clear()
        self.br.clear()


def build_program(dbg=None):
    dbg = dbg or {}
    nc = bass.Bass("TRN2", target_bir_lowering=False)
    st = ExitStack()
    P = Prog(nc, st)

    tiny = set(dbg.get('tiny', []))
    in_shapes = {}

    def din(name, shape, dt=F32):
        if name in tiny:
            shape = [2] * len(shape)
        in_shapes[name] = list(shape)
        return nc.dram_tensor(name, list(shape), dt, kind="ExternalInput").ap()

    def dscr(name, shape, dt):
        return nc.dram_tensor(name, list(shape), dt)

    x_in = din("x_in", [NTOK, D])
    x_halo = din("x_halo", [6144, D])
    mem = din("mem", [768, D])
    gT_d = din("gT", [128, 2 * 6 * KC])
    gmT_d = din("gmT", [128, 2 * KC])
    gains_d = din("norm_gains", [2, 6, D])
    w_in = din("w_in", [2, D, MIX_IN])
    w_out = din("w_out", [2, D, D])
    wsT_d = din("wsT", [128, 8, 128])
    bsT_d = din("bsT", [128, 8])
    gln_d = din("gln", [1, 1024])
    lam_d = din("c_lambda", [1, 256])
    subln_d = din("c_subln", [1, 128])
    wf_d = din("d_w_fourier", [4, 128, 128])
    wq_d = din("ca_w_q", [2, D, 512])
    wkv_d = din("ca_w_kv", [2, D, 1024])
    wo_d = din("ca_w_o", [2, 512, D])
    wup_d = din("ffn_w_up", [2, D, 2 * D_FF])
    cw_d = din("cwT", [128, 2, 88, 4])
    wdn_d = din("ffn_w_down", [2, D_FF, D])
    ident_d = din("ident", [128, 128])
    ropeA_d = din("ropeA", [NTOK + 6144, 2, 128])
    ropeC_d = din("ropeC", [NTOK, 2, 192])
    cm_d = din("cmask", [128, 17, 128])
    valid_d = din("validT", [128, 72])
    sel_d = din("sel", [54, 12])
    dft128_d = din("dft128", [128, 2, 128])
    dft32_d = din("dft32", [32, 2, 32])
    twS_d = din("twS", [128, 2, 512])
    twP_d = din("twP", [32, 2, 512])
    cwS_d = din("cwS", [128, 3, 16])
    cwP_d = din("cwP", [32, 3, 4])
    y_out = nc.dram_tensor("y", [NTOK, D], F32, kind="ExternalOutput").ap()

    xs = [dscr("xs0", [NTOK, D], F32).ap(), dscr("xs1", [NTOK, D], F32).ap()]
    mix_d = dscr("mix", [NTOK, D], BF16).ap()
    qT_d = dscr("qT", [1536, NTOK], BF16).ap()
    kTe_d = dscr("kTe", [1024, ETOT], BF16).ap()
    ve_d = dscr("ve", [ETOT, 1024], BF16).ap()
    kT1_loc = dscr("kT1_loc", [1536, NTOK], BF16)
    kT1_all = dscr("kT1_all", [NCORES * 1536, NTOK], BF16)
    v1_loc = dscr("v1_loc", [NTOK, 1536], BF16)
    v1_all = dscr("v1_all", [NCORES * NTOK, 1536], BF16)
    zd_loc = dscr("zd_loc", [NTOK, 512], BF16)
    zd_all = dscr("zd_all", [NCORES * NTOK, 512], BF16)
    bnd_loc = [dscr(f"bnd_loc{l}", [6, D], F32) for l in range(2)]
    bnd_all = [dscr(f"bnd_all{l}", [48, D], F32) for l in range(2)]

    def sb(name, shape, dt):
        return st.enter_context(nc.sbuf_tensor("s_" + name, list(shape), dt))

    def psum(name, shape, dt):
        return st.enter_context(nc.psum_tensor("p_" + name, list(shape), dt))

    ident = sb("ident", [128, 128], BF16)
    gT = sb("gT", [128, 2 * 6 * KC], F32)
    gmT = sb("gmT", [128, 2 * KC], F32)
    xn = sb("xn", [128, D], BF16)
    junk = sb("junk", [128, D], BF16)
    ssq = sb("ssq", [128, 64], F32)
    stat = sb("stat", [128, 64], F32)
    tmpf = sb("tmpf", [128, D], F32)
    B = {}
    uid = [0]

    def std_bufs(ph, want):
        uid[0] += 1
        u = uid[0]
        def a(name, shape, dt):
            return ph.enter_context(nc.sbuf_tensor("s_" + f"{name}_u{u}", list(shape), dt))
        if 'wbuf' in want:
            B['wbuf'] = [a(f"wbuf{i}", [128, 8192], BF16) for i in range(2)]
        if 'hT' in want:
            B['hT'] = a("hT", [128, KC, 512], BF16)
        if 'br' in want:
            B['br'] = a("br", [128, TG, D], BF16)
        if 'gbcA' in want:
            B['gbcA'] = a("gbcA", [128, D], F32)
        if 'gbcB' in want:
            B['gbcB'] = a("gbcB", [128, D], F32)
    psA = [psum(f"psA{i}", [128, 512], F32) for i in range(6)]
    psT = [psum(f"psT{i}", [128, 1024], BF16) for i in range(2)]

    P.dma('pool', ident[:], ident_d, w=['ident'], skey='ident')
    P.dma('sp', gT[:], gT_d, w=['gT'], skey='gT')
    P.dma('sp', gmT[:], gmT_d, w=['gmT'], skey='gmT')
    P.const.update(['ident', 'gT', 'gmT'])

    wslot = [0]
    tslot = [0]

    def rstd_from(ss_ap, out_ap, n, ndim, keys_r, key_w, mul=1.0):
        P.op('act', lambda e: e.activation(out=out_ap, in_=ss_ap, func=AF.Ln, scale=1.0 / ndim, bias=EPS_AP[0:n, :]),
             r=keys_r + ['eps'], w=[key_w])
        if mul == 1.0:
            P.op('act', lambda e: e.activation(out=out_ap, in_=out_ap, func=AF.Exp, scale=-0.5), r=[key_w], w=[key_w])
        else:
            P.op('act', lambda e: e.activation(out=out_ap, in_=out_ap, func=AF.Exp, scale=-0.5,
                                               bias=LNMUL_AP[0:n, :]), r=[key_w, 'eps'], w=[key_w])

    epsb = sb("epsb", [128, 2], F32)
    P.op('dve', lambda e: e.memset(epsb[:, 0:1], EPS), w=['eps'])
    P.op('dve', lambda e: e.memset(epsb[:, 1:2], math.log(1.0 - (0.8 - 0.6 * math.exp(-0.3)))), w=['eps'])
    EPS_AP = epsb[:, 0:1]
    LNMUL_AP = epsb[:, 1:2]
    P.const.add('eps')

    def transposes(src_fn, n_in, nblk, dst_fn, gain_fn=None, r=(), w=()):
        for b0 in range(0, nblk, 8):
            nb = min(8, nblk - b0)
            t = tslot[0] % 2
            tslot[0] += 1
            pk = f'psT{t}'
            for i in range(nb):
                b = b0 + i
                P.op('pe', lambda e, b=b, i=i: e.transpose(out=psT[t][:, i * 128:i * 128 + n_in], in_=src_fn(b),
                                                          identity=ident[0:n_in, 0:n_in]),
                     r=list(r) + ['ident'], w=[pk])
            for i in range(nb):
                b = b0 + i
                if gain_fn is not None:
                    P.op('dve', lambda e, b=b, i=i: e.tensor_scalar(out=dst_fn(b), in0=psT[t][:, i * 128:i * 128 + n_in],
                                                                    scalar1=gain_fn(b), scalar2=None, op0=ALU.mult),
                         r=[pk, 'gT', 'gmT'], w=list(w))
                else:
                    P.op('dve', lambda e, b=b, i=i: e.tensor_copy(out=dst_fn(b), in_=psT[t][:, i * 128:i * 128 + n_in]),
                         r=[pk], w=list(w))

    def norm_T(x_ap, n, xkey, gain_cols, dstT, col0, dkey):
        P.op('act', lambda e: e.activation(out=junk[0:n, :], in_=x_ap, func=AF.Square, accum_out=stat[0:n, 0:1]),
             r=[xkey], w=['junk', 'stat'])
        rstd_from(stat[0:n, 0:1], stat[0:n, 1:2], n, D, ['stat'], 'stat')
        P.op('act', lambda e: e.activation(out=xn[0:n, :], in_=x_ap, func=AF.Copy, scale=stat[0:n, 1:2]),
             r=[xkey, 'stat'], w=['xn'])
        transposes(lambda b: xn[0:n, b * 128:(b + 1) * 128], n, KC,
                   lambda b: dstT[:, b, col0:col0 + n],
                   gain_fn=lambda b: gain_cols[:, b:b + 1], r=['xn'], w=[dkey])

    def load_w(w_ap, k0, kn, n0, nn, slot_view_fn):
        s = wslot[0] % 2
        wslot[0] += 1
        view = B['wbuf'][s][:, 0:kn * nn].rearrange("p (k n) -> p k n", n=nn)
        src = w_ap[k0 * 128:(k0 + kn) * 128, n0:n0 + nn].rearrange("(k p) n -> p k n", p=128)
        P.dma('pool', view, src, w=[f'wbuf{s}'], skey=f'wbuf{s}')
        return view, f'wbuf{s}'

    def lin_tok(actT, akey, kcn, w_ap, n0, ncb, nchunks, evac, kpiece=16, ncols=512):
        npieces = (kcn + kpiece - 1) // kpiece
        for cb in range(ncb):
            for pi in range(npieces):
                k0 = pi * kpiece
                kn = min(kpiece, kcn - k0)
                wv, wk = load_w(w_ap, k0, kn, n0 + cb * ncols, ncols, None)
                for c in range(nchunks):
                    pk = f'psA{c}'
                    for kk in range(kn):
                        kc = k0 + kk
                        P.op('pe', lambda e, c=c, kc=kc, kk=kk: e.matmul(
                            psA[c][:, 0:ncols], lhsT=actT[:, kc, c * 128:(c + 1) * 128], rhs=wv[:, kk, :],
                            start=(kc == 0), stop=(kc == kcn - 1)), r=[akey, wk], w=[pk])
                    if pi == npieces - 1:
                        evac(c, cb, psA[c], pk)

    def lin_feat(actT, akey, kcn, T, w_ap, n0, nblk, evac, ps_sel=None):
        per = 4
        for b0 in range(0, nblk, per):
            nb = min(per, nblk - b0)
            wv, wk = load_w(w_ap, 0, kcn, n0 + b0 * 128, nb * 128, None)
            for i in range(nb):
                b = b0 + i
                pi = (b % 2) if ps_sel is None else ps_sel(b)
                pk = f'psA{pi}'
                for kc in range(kcn):
                    P.op('pe', lambda e, kc=kc, i=i, pi=pi: e.matmul(
                        psA[pi][:, 0:T], lhsT=wv[:, kc, i * 128:(i + 1) * 128], rhs=actT[:, kc, 0:T],
                        start=(kc == 0), stop=(kc == kcn - 1)), r=[akey, wk], w=[pk])
                evac(b, psA[pi], pk)

    def bcast_load(dst, src_row_ap, key):
        P.dma('sp', dst, src_row_ap.partition_broadcast(128), w=[key], skey=key)

    def residual_update(c, xc_ap, xkey, gb, gbkey, nss):
        P.op('dve', lambda e: e.reduce_sum(out=stat[:, 2:3], in_=ssq[:, c * 8:c * 8 + nss], axis=AX.X),
             r=['ssq'], w=['stat'])
        rstd_from(stat[:, 2:3], stat[:, 3:4], 128, D, ['stat'], 'stat')
        P.op('dve', lambda e: e.scalar_tensor_tensor(out=tmpf[:], in0=B['br'][:, c, :], scalar=stat[:, 3:4], in1=gb[:],
                                                     op0=ALU.mult, op1=ALU.mult),
             r=['br', 'stat', gbkey], w=['tmpf'])
        P.op('pool', lambda e: e.tensor_tensor(out=xc_ap, in0=xc_ap, in1=tmpf[:], op=ALU.add),
             r=['tmpf'], w=[xkey])

    def branch_evac(c, cb, ps, pk):
        P.op('act', lambda e: e.activation(out=junk[:, 0:512], in_=ps[:], func=AF.Square,
                                           accum_out=ssq[:, c * 8 + cb:c * 8 + cb + 1]), r=[pk], w=['junk', 'ssq'])
        P.op('dve', lambda e: e.tensor_copy(out=B['br'][:, c, cb * 512:(cb + 1) * 512], in_=ps[:]), r=[pk, 'ssq'], w=['br'])

    def phase_C(l, x_src, x_dst):
        with ExitStack() as ph:
            try:
                phase_C_body(l, x_src, x_dst, ph)
            except StopPhase:
                P.barrier()

    def phase_C_body(l, x_src, x_dst, ph):
        if True:
            def sbp(name, shape, dt):
                return ph.enter_context(nc.sbuf_tensor("s_" + f"{name}_{l}", list(shape), dt))
            std_bufs(ph, ['wbuf', 'hT', 'br', 'gbcA', 'gbcB'])
            hT = B['hT']; gbcA = B['gbcA']; gbcB = B['gbcB']
            xg = sbp("xgC", [128, TG, D], F32)
            otk = sbp("otk", [128, TG, 512], BF16)
            mT = sbp("mTc", [128, KC, 256], BF16)
            qTc = sbp("qTc", [128, 4, 512], BF16)
            pTc = [sbp(f"pTc{i}", [128, 2, 512], BF16) for i in range(2)]
            oT = sbp("oT", [128, 4, 512], BF16)
            mixc = [sbp(f"mixc{i}", [128, D], BF16) for i in range(2)]
            memKT = sbp("memKT", [128, 3, 4, 256], BF16)
            memV = sbp("memV", [128, 3, 2, 4, 132], BF16)
            rec = sbp("recC", [128, 8], F32)

            def cut(n):
                if dbg.get('cut') == n:
                    raise StopPhase()
            import os as _os2
            if _os2.environ.get('DBGMIX'):
                for c_ in range(NCH):
                    s_ = c_ % 2
                    P.dma('sp', xg[:, s_, :], x_src[c_ * 128:(c_ + 1) * 128, :], w=[f'xg{s_}'], skey=f'xg{s_}')
                    P.op('dve', lambda e, s_=s_: e.tensor_copy(out=mixc[s_][:], in_=xg[:, s_, :]), r=[f'xg{s_}'], w=[f'mixc{s_}'])
                    P.dma('sp', mix_d[c_ * 128:(c_ + 1) * 128, :], mixc[s_][:], r=[f'mixc{s_}'], w=[('mix', c_ // 4)],
                          skey=f'mixc{s_}')
            bcast_load(gbcA[:], gains_d[l, 1:2, :], 'gbcA')
            bcast_load(gbcB[:], gains_d[l, 3:4, :], 'gbcB')
            P.op('pool', lambda e: e.memset(memV[:], 1.0), w=['memV'])
            cut(1)
            for sq in range(3):
                for mc in range(2):
                    P.dma('sp', xg[:, mc, :], mem[sq * 256 + mc * 128:sq * 256 + (mc + 1) * 128, :], w=[f'xg{mc}'],
                          skey=f'xg{mc}')
                    norm_T(xg[:, mc, :], 128, f'xg{mc}', gmT[:, l * KC:(l + 1) * KC], mT, mc * 128, 'mT')

                def ev_k(b, ps, pk, sq=sq):
                    P.op('act', lambda e: e.activation(out=memKT[:, sq, b, :], in_=ps[:, 0:256], func=AF.Copy),
                         r=[pk], w=['memKT'])
                lin_feat(mT, 'mT', KC, 256, wkv_d[l], 0, 4, ev_k)

                def ev_v(c, cb, ps, pk, sq=sq):
                    P.op('dve', lambda e: e.tensor_copy(out=memV[:, sq, c, :, 0:128],
                                                        in_=ps[:].rearrange("p (h d) -> p h d", d=128)),
                         r=[pk], w=['memV'])
                lin_tok(mT, 'mT', KC, wkv_d[l], 512, 1, 2, ev_v)
                cut(10 + sq)
            cut(2)

            for g in range(NG):
                sq = GRP_SEG[g]
                t0 = g * 512
                for c in range(TG):
                    s = c % 2
                    import os as _os
                    _sk = _os.environ.get('SKIPA', '')
                    if 'mixdma' not in _sk:
                        P.dma('sp', mixc[s][:], mix_d[t0 + c * 128:t0 + (c + 1) * 128, :], r=[('mix', g)],
                              w=[f'mixc{s}'], skey=f'mixc{s}')
                    if 'mixtr' not in _sk:
                        transposes(lambda b, s=s: mixc[s][:, b * 128:(b + 1) * 128], 128, KC,
                                   lambda b, c=c: hT[:, b, c * 128:(c + 1) * 128], r=[f'mixc{s}'], w=['hT'])
                    if 'xdma' not in _sk:
                        P.dma('sp', xg[:, c, :], x_src[t0 + c * 128:t0 + (c + 1) * 128, :], r=[('x', l, g)],
                              w=[f'xg{c}'], skey=f'xg{c}')
                cut(3)
                lin_tok(hT, 'hT', KC, w_out[l], 0, 4, TG, branch_evac)
                cut(4)
                for c in range(TG):
                    residual_update(c, xg[:, c, :], f'xg{c}', gbcA, 'gbcA', 4)
                    norm_T(xg[:, c, :], 128, f'xg{c}', gT[:, (l * 6 + 2) * KC:(l * 6 + 3) * KC], hT, c * 128, 'hT')
                def ev_q(b, ps, pk):
                    P.op('act', lambda e: e.activation(out=qTc[:, b, :], in_=ps[:], func=AF.Copy), r=[pk], w=['qTc'])
                lin_feat(hT, 'hT', KC, 512, wq_d[l], 0, 4, ev_q)
                cut(5)
                for h in range(4):
                    pi = h % 2
                    for kt in range(2):
                        pk = f'psA{4 + kt}'
                        P.op('pe', lambda e, kt=kt, h=h: e.matmul(psA[4 + kt][:], lhsT=memKT[:, sq, h, kt * 128:(kt + 1) * 128],
                                                                  rhs=qTc[:, h, :], start=True, stop=True),
                             r=['memKT', 'qTc'], w=[pk])
                        P.op('act', lambda e, kt=kt, pi=pi: e.activation(out=pTc[pi][:, kt, :], in_=psA[4 + kt][:], func=AF.Exp,
                                                                         scale=128.0 ** -0.5), r=[pk], w=[f'pTc{pi}'])
                    for c in range(TG):
                        pk = f'psA{c}'
                        for kt in range(2):
                            P.op('pe', lambda e, kt=kt, c=c, pi=pi, h=h: e.matmul(
                                psA[c][:, 0:129], lhsT=pTc[pi][:, kt, c * 128:(c + 1) * 128],
                                rhs=memV[:, sq, kt, h, 0:129], start=(kt == 0), stop=(kt == 1)),
                                r=[f'pTc{pi}', 'memV'], w=[pk])
                        P.op('dve', lambda e, c=c: e.reciprocal(out=rec[:, c:c + 1], in_=psA[c][:, 128:129]),
                             r=[pk], w=[f'rec{c}'])
                        P.op('act', lambda e, c=c, h=h: e.activation(out=otk[:, c, h * 128:(h + 1) * 128],
                                                                     in_=psA[c][:, 0:128], func=AF.Copy,
                                                                     scale=rec[:, c:c + 1]),
                             r=[pk, f'rec{c}'], w=[f'otokb{c}'])
                cut(6)
                for c in range(TG):
                    transposes(lambda b, c=c: otk[:, c, b * 128:(b + 1) * 128], 128, 4,
                               lambda b, c=c: oT[:, b, c * 128:(c + 1) * 128], r=[f'otokb{c}'], w=['oT'])
                lin_tok(oT, 'oT', 4, wo_d[l], 0, 4, TG, branch_evac)
                for c in range(TG):
                    residual_update(c, xg[:, c, :], f'xg{c}', gbcB, 'gbcB', 4)
                    P.dma('sp', x_dst[t0 + c * 128:t0 + (c + 1) * 128, :], xg[:, c, :], r=[f'xg{c}'],
                          w=[('x2', l, g)], skey=f'xg{c}')
                cut(7)
            P.barrier()

    def phase_D(l, x_src, x_dst):
        with ExitStack() as ph:
            def sbp(name, shape, dt):
                return ph.enter_context(nc.sbuf_tensor("s_" + f"{name}_{l}", list(shape), dt))
            std_bufs(ph, ['wbuf', 'hT', 'br', 'gbcA'])
            hT = B['hT']; gbcA = B['gbcA']; wbuf = B['wbuf']
            selt = sbp("selt", [54, 12], F32)
            hTh = sbp("hTh", [128, KC, 12], BF16)
            zh = sbp("zh", [128, 88, 12], F32)
            cw = sbp("cw", [128, 88, 4], F32)
            xc = [sbp(f"xcD{i}", [128, D], F32) for i in range(2)]
            ze = [sbp(f"ze{i}", [128, 516], F32) for i in range(2)]
            tz = [sbp(f"tz{i}", [128, 512], F32) for i in range(2)]
            gg = sbp("ggD", [128, 512], F32)
            aT = sbp("aT", [128, 44, 512], BF16)
            bnd = xc[0]
            xh = xc[1]

            bcast_load(gbcA[:], gains_d[l, 5:6, :], 'gbcA')
            P.dma('sp', cw[:], cw_d[:, l, :, :], w=['cw'], skey='cw')
            P.dma('sp', selt[:], sel_d, w=['selt'], skey='selt')
            rows = [0, 511, 512, 1023, 1024, 3071]
            for i, rr in enumerate(rows):
                P.dma('sp', bnd_loc[l].ap()[i:i + 1, :], x_src[rr:rr + 1, :], w=[('bndloc', i)], skey='bndcp')
            P.collective("AllGather", bnd_loc[l].ap().opt(), bnd_all[l].ap().opt(),
                         r=[('bndloc', i) for i in range(6)], w=['bndall'], skey=f'bnd{l}')
            P.dma('sp', bnd[0:48, :], bnd_all[l].ap(), r=['bndall'], w=['xcD0'], skey='xcD0')
            irows = [1535, 1536, 2047, 2048, 2559, 2560]
            for i, rr in enumerate(irows):
                P.dma('sp', bnd[48 + i:49 + i, :], x_src[rr:rr + 1, :], w=['xcD0'], skey='xcD0')
            for cb in range(4):
                P.op('pe', lambda e, cb=cb: e.matmul(psA[cb][0:12, :], lhsT=selt[:, :], rhs=bnd[0:54, cb * 512:(cb + 1) * 512],
                                                      start=True, stop=True), r=['selt', 'xcD0'], w=[f'psA{cb}'])
                P.op('dve', lambda e, cb=cb: e.tensor_copy(out=xh[0:12, cb * 512:(cb + 1) * 512], in_=psA[cb][0:12, :]),
                     r=[f'psA{cb}'], w=['xcD1'])
            norm_T(xh[0:12, :], 12, 'xcD1', gT[:, (l * 6 + 4) * KC:(l * 6 + 5) * KC], hTh, 0, 'hTh')

            def ev_zh(b, ps, pk):
                P.op('act', lambda e: e.activation(out=zh[:, b, :], in_=ps[:, 0:12], func=AF.Copy), r=[pk], w=['zh'])
            lin_feat(hTh, 'hTh', KC, 12, wup_d[l], 0, 88, ev_zh)

            for g in range(NG):
                t0 = g * 512
                for c in range(TG):
                    s = c % 2
                    P.dma('sp', xc[s][:], x_src[t0 + c * 128:t0 + (c + 1) * 128, :], w=[f'xcD{s}'], skey=f'xcD{s}')
                    norm_T(xc[s][:], 128, f'xcD{s}', gT[:, (l * 6 + 4) * KC:(l * 6 + 5) * KC], hT, c * 128, 'hT')
                for i0 in range(0, 44, 2):
                    s = wslot[0] % 2
                    wslot[0] += 1
                    wk = f'wbuf{s}'
                    wv = wbuf[s][:].rearrange("p (k n) -> p k n", n=512)
                    srcg = wup_d[l][:, i0 * 128:(i0 + 2) * 128].rearrange("(k p) n -> p k n", p=128)
                    srcv = wup_d[l][:, D_FF + i0 * 128:D_FF + (i0 + 2) * 128].rearrange("(k p) n -> p k n", p=128)
                    P.dma('pool', wv[:, :, 0:256], srcg, w=[wk], skey=wk)
                    P.dma('pool', wv[:, :, 256:512], srcv, w=[wk], skey=wk)
                    for ii in range(2):
                        i = i0 + ii
                        for which in range(2):
                            blk = i + 44 * which
                            pi = 2 * (i % 2) + which
                            pk = f'psA{pi}'
                            zs = which
                            for kc in range(KC):
                                P.op('pe', lambda e, kc=kc, ii=ii, which=which, pi=pi: e.matmul(
                                    psA[pi][:], lhsT=wv[:, kc, which * 256 + ii * 128:which * 256 + (ii + 1) * 128],
                                    rhs=hT[:, kc, :], start=(kc == 0), stop=(kc == KC - 1)), r=['hT', wk], w=[pk])
                            P.op('act', lambda e, pi=pi, zs=zs: e.activation(out=ze[zs][:, 1:513], in_=psA[pi][:], func=AF.Copy),
                                 r=[pk], w=[f'ze{zs}'])
                            P.op('pool', lambda e, zs=zs, blk=blk: e.tensor_copy(out=ze[zs][:, 0:1], in_=zh[:, blk, 2 * g:2 * g + 1]),
                                 r=['zh'], w=[f'ze{zs}'])
                            P.op('pool', lambda e, zs=zs, blk=blk: e.tensor_copy(out=ze[zs][:, 513:514],
                                                                                in_=zh[:, blk, 2 * g + 1:2 * g + 2]),
                                 r=['zh'], w=[f'ze{zs}'])
                            P.op('dve', lambda e, zs=zs, blk=blk: e.tensor_scalar(
                                out=tz[zs][:], in0=ze[zs][:, 1:513], scalar1=cw[:, blk, 1:2], scalar2=cw[:, blk, 3:4],
                                op0=ALU.mult, op1=ALU.add), r=[f'ze{zs}', 'cw'], w=[f'tz{zs}'])
                            P.op('dve', lambda e, zs=zs, blk=blk: e.scalar_tensor_tensor(
                                out=tz[zs][:], in0=ze[zs][:, 0:512], scalar=cw[:, blk, 0:1], in1=tz[zs][:],
                                op0=ALU.mult, op1=ALU.add), r=[f'ze{zs}', 'cw', f'tz{zs}'], w=[f'tz{zs}'])
                            P.op('dve', lambda e, zs=zs, blk=blk: e.scalar_tensor_tensor(
                                out=tz[zs][:], in0=ze[zs][:, 2:514], scalar=cw[:, blk, 2:3], in1=tz[zs][:],
                                op0=ALU.mult, op1=ALU.add), r=[f'ze{zs}', 'cw', f'tz{zs}'], w=[f'tz{zs}'])
                        P.op('act', lambda e: e.activation(out=gg[:], in_=tz[0][:], func=AF.Gelu_apprx_tanh),
                             r=['tz0'], w=['gg'])
                        P.op('pool', lambda e, i=i: e.tensor_tensor(out=aT[:, i, :], in0=gg[:], in1=tz[1][:], op=ALU.mult),
                             r=['gg', 'tz1'], w=['aT'])
                lin_tok(aT, 'aT', 44, wdn_d[l], 0, 4, TG, branch_evac, kpiece=11)
                for c in range(TG):
                    s = c % 2
                    P.dma('sp', xc[s][:], x_src[t0 + c * 128:t0 + (c + 1) * 128, :], w=[f'xcD{s}'], skey=f'xcD{s}')
                    residual_update(c, xc[s][:], f'xcD{s}', gbcA, 'gbcA', 4)
                    P.dma('sp', x_dst[t0 + c * 128:t0 + (c + 1) * 128, :], xc[s][:], r=[f'xcD{s}'],
                          w=[('x3', l, g)], skey=f'xcD{s}')
            P.barrier()

    def rope(ps, ncol_heads, hd, half, cs_tile, dst, pk, dkey, cskey):
        pv = ps.rearrange("p (h d) -> p h d", d=hd)
        dv = dst.rearrange("p (h d) -> p h d", d=hd)
        cosv = cs_tile[:, 0, :].rearrange("p (h d) -> p h d", d=half)
        sinv = cs_tile[:, 1, :].rearrange("p (h d) -> p h d", d=half)
        n = ncol_heads * half
        t1 = tmpf[:, 0:n].rearrange("p (h d) -> p h d", d=half)
        t2 = tmpf[:, 512:512 + n].rearrange("p (h d) -> p h d", d=half)
        t3 = tmpf[:, 1024:1024 + n].rearrange("p (h d) -> p h d", d=half)
        t4 = tmpf[:, 1536:1536 + n].rearrange("p (h d) -> p h d", d=half)
        x1 = pv[:, :, 0:half]
        x2 = pv[:, :, half:2 * half]
        P.op('dve', lambda e: e.tensor_tensor(out=t1, in0=x1, in1=cosv, op=ALU.mult), r=[pk, cskey], w=['tmpf'])
        P.op('dve', lambda e: e.tensor_tensor(out=t2, in0=x2, in1=sinv, op=ALU.mult), r=[pk, cskey], w=['tmpf'])
        P.op('dve', lambda e: e.tensor_tensor(out=t3, in0=x2, in1=cosv, op=ALU.mult), r=[pk, cskey], w=['tmpf'])
        P.op('dve', lambda e: e.tensor_tensor(out=t4, in0=x1, in1=sinv, op=ALU.mult), r=[pk, cskey], w=['tmpf'])
        P.op('dve', lambda e: e.tensor_tensor(out=dv[:, :, 0:half], in0=t1, in1=t2, op=ALU.subtract), r=['tmpf'], w=[dkey])
        P.op('dve', lambda e: e.tensor_tensor(out=dv[:, :, half:2 * half], in0=t3, in1=t4, op=ALU.add), r=['tmpf'], w=[dkey])
        P.op('act', lambda e: e.activation(out=dv[:, :, 2 * half:hd], in_=pv[:, :, 2 * half:hd], func=AF.Copy),
             r=[pk], w=[dkey])

    def phase_0A():
        with ExitStack() as ph:
            def sbp(name, shape, dt):
                return ph.enter_context(nc.sbuf_tensor("s_" + name, list(shape), dt))
            std_bufs(ph, ['wbuf', 'hT'])
            hT = B['hT']
            xc = [sbp(f"xc0A{i}", [128, D], F32) for i in range(2)]
            csA = sbp("csA", [128, TG, 2, 128], F32)
            qkt = [sbp(f"qkt{i}", [128, 512], BF16) for i in range(2)]
            qkT = sbp("qkT0", [128, 16, 512], BF16)
            vtok = sbp("vtok0", [128, TG, 1024], BF16)
            utok = sbp("utok0", [128, TG, 1024], BF16)
            vn = sbp("vn0", [128, TG, 1024], BF16)
            vgel = [sbp(f"vgel{i}", [128, 512], F32) for i in range(2)]
            wsT = sbp("wsT", [128, 8, 128], BF16)
            bsT = sbp("bsT", [128, 8], F32)
            gln = sbp("gln", [128, 1024], F32)
            mixb = [sbp(f"mixb{i}", [128, 1024], BF16) for i in range(2)]
            lnst = sbp("lnst", [128, 16], F32)

            P.dma('pool', wsT[:], wsT_d, w=['wsT'], skey='wsT')
            P.dma('sp', bsT[:], bsT_d, w=['bsT'], skey='bsT')
            bcast_load(gln[:], gln_d[0:1, :], 'gln')
            g0 = gT[:, 0:KC]

            def do_group(rows_ap, rope_row0, main, g, ecol):
                for c in range(TG):
                    s = c % 2
                    P.dma('sp', xc[s][:], rows_ap[c * 128:(c + 1) * 128, :], w=[f'xc0A{s}'], skey=f'xc0A{s}')
                    norm_T(xc[s][:], 128, f'xc0A{s}', g0, hT, c * 128, 'hT')
                P.dma('sp', csA[:].rearrange("p c t f -> p c (t f)"),
                      ropeA_d[rope_row0:rope_row0 + 512, :, :].rearrange("(c p) t f -> p c (t f)", p=128),
                      w=['csA'], skey='csA')

                def evac(c, cb, ps, pk):
                    cbt = cb if main else cb + 2
                    if cbt < 4:
                        s = (c + cb) % 2
                        rope(ps[:], 4, 128, 16, csA[:, c, :, (cbt % 2) * 64:(cbt % 2) * 64 + 64], qkt[s][:], pk, f'qkt{s}', 'csA')
                        transposes(lambda b, s=s: qkt[s][:, b * 128:(b + 1) * 128], 128, 4,
                                   lambda b, c=c, cbt=cbt: qkT[:, cbt * 4 + b, c * 128:(c + 1) * 128],
                                   r=[f'qkt{s}'], w=['qkT0'])
                    elif cbt < 6:
                        P.op('act', lambda e: e.activation(out=vtok[:, c, (cbt - 4) * 512:(cbt - 3) * 512], in_=ps[:], func=AF.Copy),
                             r=[pk], w=['vtok0'])
                    elif cbt < 8:
                        P.op('act', lambda e: e.activation(out=utok[:, c, (cbt - 6) * 512:(cbt - 5) * 512], in_=ps[:],
                                                           func=AF.Gelu_apprx_tanh), r=[pk], w=['utok0'])
                    else:
                        s = (c + cb) % 2
                        vg = vgel[s]
                        vk = f'vgel{s}'
                        P.op('act', lambda e: e.activation(out=vg[:], in_=ps[:], func=AF.Gelu_apprx_tanh), r=[pk], w=[vk])
                        v3 = vg[:].rearrange("p (g d) -> p g d", d=128)
                        P.op('dve', lambda e: e.reduce_sum(out=lnst[:, 0:4], in_=v3, axis=AX.X), r=[vk], w=['lnst'])
                        P.op('pool', lambda e: e.tensor_tensor(out=tmpf[:, 0:512], in0=vg[:], in1=vg[:], op=ALU.mult),
                             r=[vk], w=['tmpf'])
                        P.op('dve', lambda e: e.reduce_sum(out=lnst[:, 4:8],
                                                           in_=tmpf[:, 0:512].rearrange("p (g d) -> p g d", d=128), axis=AX.X),
                             r=['tmpf'], w=['lnst'])
                        P.op('dve', lambda e: e.tensor_scalar(out=lnst[:, 0:4], in0=lnst[:, 0:4], scalar1=1.0 / 128, scalar2=None,
                                                              op0=ALU.mult), r=['lnst'], w=['lnst'])
                        P.op('dve', lambda e: e.tensor_tensor(out=lnst[:, 8:12], in0=lnst[:, 0:4], in1=lnst[:, 0:4], op=ALU.mult),
                             r=['lnst'], w=['lnst'])
                        P.op('dve', lambda e: e.scalar_tensor_tensor(out=lnst[:, 4:8], in0=lnst[:, 4:8], scalar=1.0 / 128,
                                                                     in1=lnst[:, 8:12], op0=ALU.mult, op1=ALU.subtract),
                             r=['lnst'], w=['lnst'])
                        rstd_from(lnst[:, 4:8], lnst[:, 12:16], 128, 1.0, ['lnst'], 'lnst')
                        for gi in range(4):
                            P.op('dve', lambda e, gi=gi: e.tensor_scalar(out=vg[:, gi * 128:(gi + 1) * 128],
                                                                         in0=vg[:, gi * 128:(gi + 1) * 128],
                                                                         scalar1=lnst[:, gi:gi + 1], scalar2=lnst[:, 12 + gi:13 + gi],
                                                                         op0=ALU.subtract, op1=ALU.mult),
                                 r=[vk, 'lnst'], w=[vk])
                        P.op('pool', lambda e: e.tensor_tensor(out=vn[:, c, (cbt - 8) * 512:(cbt - 7) * 512], in0=vg[:],
                                                               in1=gln[:, (cbt - 8) * 512:(cbt - 7) * 512], op=ALU.mult),
                             r=[vk, 'gln'], w=['vn0'])
                if main:
                    lin_tok(hT, 'hT', KC, w_in[0], 0, 10, TG, evac)
                else:
                    lin_tok(hT, 'hT', KC, w_in[0], 1024, 4, TG, evac)
                if main:
                    t0 = g * 512
                    P.dma('sp', qT_d[0:1024, t0:t0 + 512].rearrange("(h p) t -> p h t", p=128), qkT[:, 0:8, :],
                          r=['qkT0'], w=[('qT', g)], skey='qkT0')
                P.dma('sp', kTe_d[:, ecol:ecol + 512].rearrange("(h p) t -> p h t", p=128), qkT[:, 8:16, :],
                      r=['qkT0'], w=[('kTe', ecol)], skey='qkT0')
                P.dma('sp', ve_d[ecol:ecol + 512, :].rearrange("(c p) f -> p c f", p=128), vtok[:], r=['vtok0'],
                      w=[('ve', ecol)], skey='vtok0')
                if main:
                    for c in range(TG):
                        s = c % 2
                        for gi in range(8):
                            pi = 4 + (gi // 4)
                            P.op('pe', lambda e, gi=gi, c=c, pi=pi: e.matmul(
                                psA[pi][:, (gi % 4) * 128:(gi % 4 + 1) * 128], lhsT=wsT[:, gi, :],
                                rhs=vn[:, c, gi * 128:(gi + 1) * 128], start=True, stop=True),
                                r=['wsT', 'vn0'], w=[f'psA{pi}'])
                        for gi in range(8):
                            pi = 4 + (gi // 4)
                            P.op('dve', lambda e, gi=gi, c=c, pi=pi, s=s: e.scalar_tensor_tensor(
                                out=mixb[s][:, gi * 128:(gi + 1) * 128], in0=psA[pi][:, (gi % 4) * 128:(gi % 4 + 1) * 128],
                                scalar=bsT[:, gi:gi + 1], in1=utok[:, c, gi * 128:(gi + 1) * 128], op0=ALU.add, op1=ALU.mult),
                                r=[f'psA{pi}', 'bsT', 'utok0'], w=[f'mixb{s}'])
                        P.dma('sp', mix_d[t0 + c * 128:t0 + (c + 1) * 128, 1024:2048], mixb[s][:], r=[f'mixb{s}'],
                              w=[('mixB', g, c)], skey=f'mixb{s}')

            for g in range(NG):
                sq = GRP_SEG[g]
                ecol = EOFF[sq] + 1024 + (g * 512 - SEG_OFF[sq])
                do_group(x_in[g * 512:(g + 1) * 512, :], g * 512, True, g, ecol)
            for hg in range(12):
                sq = hg // 4
                part = hg % 4
                ecol = EOFF[sq] + part * 512 if part < 2 else EOFF[sq] + 1024 + SEG_LEN[sq] + (part - 2) * 512
                do_group(x_halo[hg * 512:(hg + 1) * 512, :], NTOK + hg * 512, False, None, ecol)
            P.barrier()

    def phase_0B():
        with ExitStack() as ph:
            def sbp(name, shape, dt):
                return ph.enter_context(nc.sbuf_tensor("s_" + name, list(shape), dt))
            cm = sbp("cm", [128, 17, 128], BF16)
            validT = sbp("validT", [128, 72], F32)
            qTs = [sbp(f"qTs{i}", [128, 2048], BF16) for i in range(2)]
            pE = [sbp(f"pE{i}", [128, 4, 128], BF16) for i in range(2)]
            pM = [sbp(f"pM{i}", [128, 4, 128], BF16) for i in range(2)]
            ya = [sbp(f"ya{i}", [128, 16, 128], BF16) for i in range(2)]
            rec = sbp("rec0B", [128, 2], F32)
            kv0 = [sbp(f"kv0_{i}", [128, 4096 + 32 * 132], BF16) for i in range(2)]
            P.dma('pool', cm[:], cm_d, w=['cm'], skey='cm')
            P.dma('sp', validT[:], valid_d, w=['validT'], skey='validT')
            P.const.update(['cm', 'validT'])
            it = 0
            for sq in range(3):
                L = SEG_LEN[sq]
                EL = L + 2048
                ntile = EL // 128
                tb = EOFF[sq] // 128
                for h in range(8):
                    s = it % 2
                    it += 1
                    kT = kv0[s][:, 0:EL]
                    vE = kv0[s][:, 4096:4096 + ntile * 132].rearrange("p (t f) -> p t f", f=132)
                    kk = f'kTs{s}'
                    vk = f'vEs{s}'
                    P.dma('sp', kT, kTe_d[h * 128:(h + 1) * 128, EOFF[sq]:EOFF[sq] + EL], w=[kk], skey=kk)
                    P.dma('sp', vE[:, :, 0:128],
                          ve_d[EOFF[sq]:EOFF[sq] + EL, h * 128:(h + 1) * 128].rearrange("(t p) f -> p t f", p=128),
                          w=[vk], skey=vk)
                    P.op('pool', lambda e, vE=vE, tb=tb, ntile=ntile: e.tensor_copy(
                        out=vE[:, :, 128:129], in_=validT[:, tb:tb + ntile].rearrange("p (t o) -> p t o", o=1)),
                        r=['validT'], w=[vk])
                    P.dma('sp', qTs[s][:, 0:L], qT_d[h * 128:(h + 1) * 128, SEG_OFF[sq]:SEG_OFF[sq] + L], w=[f'qTs{s}'],
                          skey=f'qTs{s}')
                    for j in range(L // 128):
                        po = 4 + (j % 2)
                        pok = f'psA{po}'
                        nb = 0
                        for b0 in range(0, 17, 4):
                            nt = min(4, 17 - b0)
                            pi = nb % 4
                            e2 = nb % 2
                            nb += 1
                            pk = f'psA{pi}'
                            for i in range(nt):
                                kt = b0 + i
                                P.op('pe', lambda e, i=i, kt=kt, pi=pi, kT=kT, s=s, j=j: e.matmul(
                                    psA[pi][:, i * 128:(i + 1) * 128], lhsT=kT[:, (j + kt) * 128:(j + kt + 1) * 128],
                                    rhs=qTs[s][:, j * 128:(j + 1) * 128], start=True, stop=True),
                                    r=[kk, f'qTs{s}'], w=[pk])
                            P.op('act', lambda e, pi=pi, e2=e2, nt=nt: e.activation(
                                out=pE[e2][:, 0:nt, :], in_=psA[pi][:, 0:nt * 128].rearrange("p (t q) -> p t q", q=128),
                                func=AF.Exp, scale=128.0 ** -0.5), r=[pk], w=[f'pE{e2}'])
                            P.op('dve', lambda e, e2=e2, nt=nt, b0=b0: e.tensor_tensor(
                                out=pM[e2][:, 0:nt, :], in0=pE[e2][:, 0:nt, :], in1=cm[:, b0:b0 + nt, :], op=ALU.mult),
                                r=[f'pE{e2}', 'cm'], w=[f'pM{e2}'])
                            for i in range(nt):
                                kt = b0 + i
                                P.op('pe', lambda e, i=i, kt=kt, e2=e2, vE=vE, po=po, j=j: e.matmul(
                                    psA[po][:, 0:129], lhsT=pM[e2][:, i, :], rhs=vE[:, j + kt, 0:129],
                                    start=(kt == 0), stop=(kt == 16)), r=[f'pM{e2}', vk], w=[pok])
                        P.op('dve', lambda e, po=po, j=j: e.reciprocal(out=rec[:, j % 2:j % 2 + 1], in_=psA[po][:, 128:129]),
                             r=[pok], w=[f'rec{j % 2}'])
                        P.op('act', lambda e, po=po, j=j, s=s: e.activation(out=ya[s][:, j, :], in_=psA[po][:, 0:128],
                                                                            func=AF.Copy, scale=rec[:, j % 2:j % 2 + 1]),
                             r=[pok, f'rec{j % 2}'], w=[f'ya{s}'])
                    P.dma('sp', mix_d[SEG_OFF[sq]:SEG_OFF[sq] + L, h * 128:(h + 1) * 128].rearrange("(j p) f -> p j f", p=128),
                          ya[s][:, 0:L // 128, :], r=[f'ya{s}'], w=[('mixA', sq, h)], skey=f'ya{s}')
            P.barrier()

    def phase_1A(x_src):
        with ExitStack() as ph:
            def sbp(name, shape, dt):
                return ph.enter_context(nc.sbuf_tensor("s_" + name, list(shape), dt))
            std_bufs(ph, ['wbuf', 'hT'])
            hT = B['hT']
            xc = [sbp(f"xc1A{i}", [128, D], F32) for i in range(2)]
            csC = sbp("csC", [128, TG, 2, 192], F32)
            qkt = [sbp(f"qkt1{i}", [128, 512], BF16) for i in range(2)]
            qkT = sbp("qkT1", [128, 24, 512], BF16)
            vtok = sbp("vtok1", [128, TG, 1536], BF16)
            ztok = sbp("ztok1", [128, TG, 512], BF16)
            g0 = gT[:, 6 * KC:7 * KC]
            for g in range(NG):
                t0 = g * 512
                for c in range(TG):
                    s = c % 2
                    P.dma('sp', xc[s][:], x_src[t0 + c * 128:t0 + (c + 1) * 128, :], w=[f'xc1A{s}'], skey=f'xc1A{s}')
                    norm_T(xc[s][:], 128, f'xc1A{s}', g0, hT, c * 128, 'hT')
                P.dma('sp', csC[:].rearrange("p c t f -> p c (t f)"),
                      ropeC_d[t0:t0 + 512, :, :].rearrange("(c p) t f -> p c (t f)", p=128), w=['csC'], skey='csC')

                def evac(c, cb, ps, pk):
                    if cb < 6:
                        s = (c + cb) % 2
                        rope(ps[:], 8, 64, 8, csC[:, c, :, (cb % 3) * 64:(cb % 3) * 64 + 64], qkt[s][:], pk, f'qkt1{s}', 'csC')
                        transposes(lambda b, s=s: qkt[s][:, b * 128:(b + 1) * 128], 128, 4,
                                   lambda b, c=c, cb=cb: qkT[:, cb * 4 + b, c * 128:(c + 1) * 128],
                                   r=[f'qkt1{s}'], w=['qkT1'])
                    elif cb < 9:
                        P.op('act', lambda e: e.activation(out=vtok[:, c, (cb - 6) * 512:(cb - 5) * 512], in_=ps[:], func=AF.Copy),
                             r=[pk], w=['vtok1'])
                    else:
                        P.op('act', lambda e: e.activation(out=ztok[:, c, :], in_=ps[:], func=AF.Copy), r=[pk], w=['ztok1'])
                lin_tok(hT, 'hT', KC, w_in[1], 0, 10, TG, evac)
                P.dma('sp', qT_d[:, t0:t0 + 512].rearrange("(h p) t -> p h t", p=128), qkT[:, 0:12, :],
                      r=['qkT1'], w=[('qT1', g)], skey='qkT1')
                P.dma('sp', kT1_loc.ap()[:, t0:t0 + 512].rearrange("(h p) t -> p h t", p=128), qkT[:, 12:24, :],
                      r=['qkT1'], w=[('kT1', g)], skey='qkT1')
                P.dma('sp', v1_loc.ap()[t0:t0 + 512, :].rearrange("(c p) f -> p c f", p=128), vtok[:], r=['vtok1'],
                      w=[('v1', g)], skey='vtok1')
                P.dma('sp', zd_loc.ap()[t0:t0 + 512, :].rearrange("(c p) f -> p c f", p=128), ztok[:], r=['ztok1'],
                      w=[('zd', g)], skey='ztok1')
            P.collective("AllGather", kT1_loc.ap().opt(), kT1_all.ap().opt(), r=[('kT1', g) for g in range(NG)],
                         w=['kT1_all'], skey='kT1')
            P.collective("AllGather", v1_loc.ap().opt(), v1_all.ap().opt(), r=[('v1', g) for g in range(NG)],
                         w=['v1_all'], skey='v1')
            P.collective("AllGather", zd_loc.ap().opt(), zd_all.ap().opt(), r=[('zd', g) for g in range(NG)],
                         w=['zd_all'], skey='zd')
            P.barrier()

    def phase_1B():
        lam_init = 0.8 - 0.6 * math.exp(-0.3 * 1)
        with ExitStack() as ph:
            def sbp(name, shape, dt):
                return ph.enter_context(nc.sbuf_tensor("s_" + name, list(shape), dt))
            lamt = sbp("lamt", [128, 256], F32)
            lst = sbp("lst", [128, 8], F32)
            subg = sbp("subg", [128, 128], F32)
            qTa = [sbp(f"qTa1{i}", [128, 2048], BF16) for i in range(2)]
            qTb = [sbp(f"qTb1{i}", [128, 2048], BF16) for i in range(2)]
            for i in range(2):
                P.op('pool', lambda e, i=i: e.memset(qTa[i][64:128, :], 0.0), w=[f'qTs1{i}'])
                P.op('pool', lambda e, i=i: e.memset(qTb[i][0:64, :], 0.0), w=[f'qTs1{i}'])
            pT = [sbp(f"pT1{i}", [128, 512], BF16) for i in range(3)]
            yc = [sbp(f"yc{i}", [128, 16, 128], BF16) for i in range(2)]
            of = sbp("of1", [128, 128], F32)
            rc = sbp("rc1", [128, 4], F32)
            kbuf = [sbp(f"kbuf{i}", [128, 16384], BF16) for i in range(2)]
            vbuf = [sbp(f"vbuf{i}", [128, 128 * 130], BF16) for i in range(2)]
            for i in range(2):
                P.op('pool', lambda e, i=i: e.memset(vbuf[i][:], 1.0), w=[f'v1s{i}'])
            bcast_load(lamt[:], lam_d[0:1, :], 'lamt')
            bcast_load(subg[:], subln_d[0:1, :], 'subg')
            P.op('dve', lambda e: e.tensor_tensor(out=tmpf[:, 0:64], in0=lamt[:, 0:64], in1=lamt[:, 64:128], op=ALU.mult),
                 r=['lamt'], w=['tmpf'])
            P.op('dve', lambda e: e.tensor_tensor(out=tmpf[:, 64:128], in0=lamt[:, 128:192], in1=lamt[:, 192:256], op=ALU.mult),
                 r=['lamt'], w=['tmpf'])
            P.op('dve', lambda e: e.reduce_sum(out=lst[:, 0:2], in_=tmpf[:, 0:128].rearrange("p (a d) -> p a d", d=64), axis=AX.X),
                 r=['tmpf'], w=['lst'])
            P.op('act', lambda e: e.activation(out=lst[:, 2:4], in_=lst[:, 0:2], func=AF.Exp), r=['lst'], w=['lst'])
            P.op('dve', lambda e: e.scalar_tensor_tensor(out=lst[:, 4:5], in0=lst[:, 3:4], scalar=-lam_init, in1=lst[:, 2:3],
                                                         op0=ALU.add, op1=ALU.subtract), r=['lst'], w=['lst'])
            P.const.update(['subg'])
            vcols = 129 * 128 + 128
            it = 0
            for sq in dbg.get('segs1B', range(3)):
                L = SEG_LEN[sq]
                S = SEG_S[sq]
                nkt = S // 128
                for h in range(12):
                    s = it % 2
                    it += 1
                    kk = f'k1s{s}'
                    vk = f'v1s{s}'
                    kT = kbuf[s][:, 0:S]
                    vE = vbuf[s][:, 0:nkt * 130].rearrange("p (t f) -> p t f", f=130)
                    for r_ in range(NCORES):
                        P.dma('sp', kT[:, r_ * L:(r_ + 1) * L],
                              kT1_all.ap()[r_ * 1536 + h * 128:r_ * 1536 + (h + 1) * 128, SEG_OFF[sq]:SEG_OFF[sq] + L],
                              r=['kT1_all'], w=[kk], skey=kk)
                        P.dma('sp', vE[:, r_ * (L // 128):(r_ + 1) * (L // 128), 0:128],
                              v1_all.ap()[r_ * NTOK + SEG_OFF[sq]:r_ * NTOK + SEG_OFF[sq] + L, h * 128:(h + 1) * 128]
                              .rearrange("(t p) f -> p t f", p=128), r=['v1_all'], w=[vk], skey=vk)
                    P.dma('sp', qTa[s][0:64, 0:L], qT_d[h * 128:h * 128 + 64, SEG_OFF[sq]:SEG_OFF[sq] + L], w=[f'qTs1{s}'],
                          skey=f'qTs1{s}')
                    P.dma('sp', qTb[s][64:128, 0:L], qT_d[h * 128 + 64:(h + 1) * 128, SEG_OFF[sq]:SEG_OFF[sq] + L],
                          w=[f'qTs1{s}'], skey=f'qTs1{s}')
                    rnd = 0
                    for qg in range(L // 256):
                        q0 = qg * 256
                        for kt in range(nkt):
                            sp_ = 4 + (rnd % 2)
                            pt_ = rnd % 3
                            rnd += 1
                            spk = f'psA{sp_}'
                            for half in range(2):
                                P.op('pe', lambda e, half=half, kt=kt, sp_=sp_, kT=kT, s=s, q0=q0: e.matmul(
                                    psA[sp_][:, half * 256:(half + 1) * 256],
                                    lhsT=kT[:, kt * 128:(kt + 1) * 128],
                                    rhs=(qTa[s] if half == 0 else qTb[s])[:, q0:q0 + 256], start=True, stop=True),
                                    r=[kk, f'qTs1{s}'], w=[spk])
                            P.op('act', lambda e, sp_=sp_, pt_=pt_: e.activation(out=pT[pt_][:], in_=psA[sp_][:], func=AF.Exp,
                                                                               scale=64.0 ** -0.5), r=[spk], w=[f'pT1{pt_}'])
                            for half in range(2):
                                for c in range(2):
                                    oi = half * 2 + c
                                    P.op('pe', lambda e, half=half, c=c, oi=oi, pt_=pt_, vE=vE, kt=kt: e.matmul(
                                        psA[oi][:, 0:129], lhsT=pT[pt_][:, half * 256 + c * 128:half * 256 + (c + 1) * 128],
                                        rhs=vE[:, kt, 0:129], start=(kt == 0), stop=(kt == nkt - 1)),
                                        r=[f'pT1{pt_}', vk], w=[f'psA{oi}'])
                        for c in range(2):
                            j = qg * 2 + c
                            P.op('dve', lambda e, c=c: e.reciprocal(out=rc[:, 0:1], in_=psA[c][:, 128:129]), r=[f'psA{c}'], w=['rc1'])
                            P.op('dve', lambda e, c=c: e.reciprocal(out=rc[:, 1:2], in_=psA[2 + c][:, 128:129]), r=[f'psA{2 + c}'],
                                 w=['rc1'])
                            P.op('dve', lambda e: e.tensor_tensor(out=rc[:, 1:2], in0=rc[:, 1:2], in1=lst[:, 4:5], op=ALU.mult),
                                 r=['rc1', 'lst'], w=['rc1'])
                            P.op('act', lambda e, c=c: e.activation(out=of[:], in_=psA[c][:, 0:128], func=AF.Copy, scale=rc[:, 0:1]),
                                 r=[f'psA{c}', 'rc1'], w=['of1'])
                            P.op('dve', lambda e, c=c: e.scalar_tensor_tensor(out=of[:], in0=psA[2 + c][:, 0:128], scalar=rc[:, 1:2],
                                                                              in1=of[:], op0=ALU.mult, op1=ALU.add),
                                 r=[f'psA{2 + c}', 'rc1', 'of1'], w=['of1'])
                            P.op('act', lambda e: e.activation(out=junk[:, 0:128], in_=of[:], func=AF.Square, accum_out=rc[:, 2:3]),
                                 r=['of1'], w=['junk', 'rc1'])
                            rstd_from(rc[:, 2:3], rc[:, 3:4], 128, 128.0, ['rc1'], 'rc1', mul=(1.0 - lam_init))
                            P.op('dve', lambda e, j=j, s=s: e.scalar_tensor_tensor(out=yc[s][:, j, :], in0=of[:], scalar=rc[:, 3:4],
                                                                                  in1=subg[:], op0=ALU.mult, op1=ALU.mult),
                                 r=['of1', 'rc1', 'subg'], w=[f'yc{s}'])
                    P.dma('sp', mix_d[SEG_OFF[sq]:SEG_OFF[sq] + L, h * 128:(h + 1) * 128].rearrange("(j p) f -> p j f", p=128),
                          yc[s][:, 0:L // 128, :], r=[f'yc{s}'], w=[('mixC', sq, h)], skey=f'yc{s}')
            P.barrier()

    def phase_1F():
        with ExitStack() as ph:
            def sbp(name, shape, dt):
                return ph.enter_context(nc.sbuf_tensor("s_" + name, list(shape), dt))
            dft128 = sbp("dft128", [128, 2, 128], BF16)
            dft32 = sbp("dft32", [32, 2, 32], BF16)
            twS = sbp("twS", [128, 2, 512], F32)
            twP = sbp("twP", [32, 2, 512], F32)
            cwS = sbp("cwS", [128, 3, 16], BF16)
            cwP = sbp("cwP", [32, 3, 4], BF16)
            wf = sbp("wf", [128, 4, 128], BF16)
            AB = sbp("AB", [128, 2, 4, 128], BF16)
            t1 = [sbp(f"ft{i}", [128, 512], F32) for i in range(4)]
            utok = [sbp(f"futok{i}", [128, 2, 128], BF16) for i in range(2)]
            uT = [sbp(f"fuT{i}", [128, 2, 128], BF16) for i in range(2)]
            ydt = [sbp(f"fyd{i}", [128, 128], BF16) for i in range(2)]
            P.dma('pool', dft128[:], dft128_d, w=['dft128'], skey='dft128')
            P.dma('pool', dft32[:], dft32_d, w=['dft32'], skey='dft32')
            P.dma('sp', twS[:], twS_d, w=['twS'], skey='twS')
            P.dma('sp', twP[:], twP_d, w=['twP'], skey='twP')
            P.dma('pool', cwS[:], cwS_d, w=['cwS'], skey='cwS')
            P.dma('pool', cwP[:], cwP_d, w=['cwP'], skey='cwP')
            P.dma('pool', wf[:], wf_d.rearrange("g c e -> c g e"), w=['wf'], skey='wf')
            P.const.update(['dft128', 'dft32', 'twS', 'twP', 'cwS', 'cwP', 'wf'])
            for g in range(4):
                for t in range(2):
                    pk = f'psA{t}'
                    P.op('pe', lambda e, g=g, t=t: e.matmul(psA[t][:, 0:128], lhsT=dft128[:, t, :], rhs=wf[:, g, :],
                                                            start=True, stop=True), r=['dft128', 'wf'], w=[pk])
                    P.op('act', lambda e, g=g, t=t: e.activation(out=AB[:, t, g, :], in_=psA[t][:, 0:128], func=AF.Copy),
                         r=[pk], w=['AB'])
            xin = sbp("fxin", [128, 16384], BF16)
            Tr = sbp("fTr", [128, 16384], BF16)
            Ti = sbp("fTi", [128, 16384], BF16)
            ust = sbp("ust", [128, 2, 16, 128], BF16)
            for sq in range(3):
                L = SEG_LEN[sq]
                N1 = SEG_S[sq] // 128
                nch = L // 128
                dftN = dft128 if N1 == 128 else dft32
                dk = 'dft128' if N1 == 128 else 'dft32'
                tw = twS if N1 == 128 else twP
                twk = 'twS' if N1 == 128 else 'twP'
                cwt = cwS if N1 == 128 else cwP
                cwk = 'cwS' if N1 == 128 else 'cwP'
                for g in range(4):
                    xv = xin[:, 0:N1 * 128].rearrange("p (s c) -> p s c", c=128)
                    Trv = Tr[0:N1, :].rearrange("p (c k) -> p c k", k=128)
                    Tiv = Ti[0:N1, :].rearrange("p (c k) -> p c k", k=128)
                    for r_ in range(NCORES):
                        src = zd_all.ap()[r_ * NTOK + SEG_OFF[sq]:r_ * NTOK + SEG_OFF[sq] + L, g * 128:(g + 1) * 128] \
                            .rearrange("(a s) c -> a s c", s=N1)
                        P.dma('sp', xv[r_ * 16:(r_ + 1) * 16, :, :], src, r=['zd_all'], w=['xin'], skey='xin')
                    for c0 in range(0, 128, 4):
                        for t in range(2):
                            pk = f'psA{t}'
                            for ci in range(4):
                                P.op('pe', lambda e, t=t, ci=ci, c0=c0: e.matmul(
                                    psA[t][0:N1, ci * 128:(ci + 1) * 128], lhsT=xv[:, :, c0 + ci], rhs=dft128[:, t, :],
                                    start=True, stop=True), r=['xin', 'dft128'], w=[pk])
                        P.op('dve', lambda e: e.tensor_tensor(out=t1[0][0:N1, :], in0=psA[0][0:N1, :], in1=tw[0:N1, 0, :], op=ALU.mult),
                             r=['psA0', twk], w=['ft0'])
                        P.op('dve', lambda e: e.tensor_tensor(out=t1[1][0:N1, :], in0=psA[1][0:N1, :], in1=tw[0:N1, 1, :], op=ALU.mult),
                             r=['psA1', twk], w=['ft1'])
                        P.op('dve', lambda e: e.tensor_tensor(out=t1[2][0:N1, :], in0=psA[0][0:N1, :], in1=tw[0:N1, 1, :], op=ALU.mult),
                             r=['psA0', twk], w=['ft2'])
                        P.op('dve', lambda e: e.tensor_tensor(out=t1[3][0:N1, :], in0=psA[1][0:N1, :], in1=tw[0:N1, 0, :], op=ALU.mult),
                             r=['psA1', twk], w=['ft3'])
                        P.op('pool', lambda e, c0=c0: e.tensor_tensor(out=Trv[:, c0:c0 + 4, :],
                                                                      in0=t1[0][0:N1, :].rearrange("p (c k) -> p c k", k=128),
                                                                      in1=t1[1][0:N1, :].rearrange("p (c k) -> p c k", k=128),
                                                                      op=ALU.subtract), r=['ft0', 'ft1'], w=['Tr'])
                        P.op('pool', lambda e, c0=c0: e.tensor_tensor(out=Tiv[:, c0:c0 + 4, :],
                                                                      in0=t1[2][0:N1, :].rearrange("p (c k) -> p c k", k=128),
                                                                      in1=t1[3][0:N1, :].rearrange("p (c k) -> p c k", k=128),
                                                                      op=ALU.add), r=['ft2', 'ft3'], w=['Ti'])
                    for hc in range(2):
                        for c in range(64):
                            cc = hc * 64 + c
                            col = c * nch
                            bank = col // 512
                            off = col % 512
                            P.op('pe', lambda e, cc=cc, bank=bank, off=off: e.matmul(
                                psA[bank][:, off:off + nch], lhsT=Trv[:, cc, :], rhs=cwt[0:N1, 0, 0:nch], start=True, stop=False),
                                r=['Tr', cwk], w=[f'psA{bank}'])
                            P.op('pe', lambda e, cc=cc, bank=bank, off=off: e.matmul(
                                psA[bank][:, off:off + nch], lhsT=Tiv[:, cc, :], rhs=cwt[0:N1, 1, 0:nch], start=False, stop=True),
                                r=['Ti', cwk], w=[f'psA{bank}'])
                            P.op('pe', lambda e, cc=cc, bank=bank, off=off: e.matmul(
                                psA[2 + bank][:, off:off + nch], lhsT=Tiv[:, cc, :], rhs=cwt[0:N1, 2, 0:nch], start=True, stop=False),
                                r=['Ti', cwk], w=[f'psA{2 + bank}'])
                            P.op('pe', lambda e, cc=cc, bank=bank, off=off: e.matmul(
                                psA[2 + bank][:, off:off + nch], lhsT=Trv[:, cc, :], rhs=cwt[0:N1, 1, 0:nch], start=False, stop=True),
                                r=['Tr', cwk], w=[f'psA{2 + bank}'])
                        ncol = 64 * nch
                        for t in range(2):
                            for bk in range((ncol + 511) // 512):
                                w_ = min(512, ncol - bk * 512)
                                nc_ = w_ // nch
                                P.op('act', lambda e, t=t, bk=bk, w_=w_, nc_=nc_, hc=hc: e.activation(
                                    out=ust[:, t, 0:nch, hc * 64 + bk * (512 // nch):hc * 64 + bk * (512 // nch) + nc_]
                                    .rearrange("p k c -> p c k"),
                                    in_=psA[2 * t + bk][:, 0:w_].rearrange("p (c k) -> p c k", k=nch), func=AF.Copy),
                                    r=[f'psA{2 * t + bk}'], w=['ust'])
                    for k1 in range(nch):
                        s = k1 % 2
                        transposes(lambda b, k1=k1: ust[:, b, k1, :], 128, 2, lambda b, s=s: uT[s][:, b, :],
                                   r=['ust'], w=[f'fuT{s}'])
                        pk = f'psA{4 + s}'
                        for t in range(2):
                            P.op('pe', lambda e, t=t, s=s, g=g: e.matmul(psA[4 + s][:, 0:128], lhsT=uT[s][:, t, :], rhs=AB[:, t, g, :],
                                                                          start=(t == 0), stop=(t == 1)),
                                 r=[f'fuT{s}', 'AB'], w=[pk])
                        P.op('act', lambda e, s=s: e.activation(out=ydt[s][:], in_=psA[4 + s][:, 0:128], func=AF.Copy),
                             r=[pk], w=[f'fyd{s}'])
                        tok0 = SEG_OFF[sq] + k1 * 128
                        P.dma('sp', mix_d[tok0:tok0 + 128, 1536 + g * 128:1536 + (g + 1) * 128], ydt[s][:], r=[f'fyd{s}'],
                              w=[('mixD', sq, g, k1)], skey=f'fyd{s}')
            P.barrier()


    phases = dbg.get('phases', ['0A', '0B', 'C0', 'D0', '1A', '1B', '1F', 'C1', 'D1'])
    P.barrier()
    if '0A' in phases:
        phase_0A()
    if '0B' in phases:
        phase_0B()
    if 'C0' in phases:
        phase_C(0, x_in, xs[0])
    if 'D0' in phases:
        phase_D(0, xs[0], xs[1])
    if '1A' in phases:
        phase_1A(xs[1])
    if '1B' in phases:
        phase_1B()
    if '1F' in phases:
        phase_1F()
    if 'C1' in phases:
        phase_C(1, xs[1], xs[0])
    if 'D1' in phases:
        phase_D(1, xs[0], y_out)
    dumps = dbg.get('dump', [])
    outs = {}
    if dumps:
        dt_ = sb("dbgt", [128, D], F32)
        for name in dumps:
            src = {'mix': mix_d, 'xs0': xs[0], 'xs1': xs[1]}[name]
            o = nc.dram_tensor("dbg_" + name, [NTOK, D], F32, kind="ExternalOutput").ap()
            for c in range(NCH):
                if name == 'mix':
                    P.dma('pool', dt_[:], src[c * 128:(c + 1) * 128, :], w=['dbgt'], skey='dbgt')
                else:
                    P.dma('sp', dt_[:], src[c * 128:(c + 1) * 128, :], w=['dbgt'], skey='dbgt')
                P.dma('sp', o[c * 128:(c + 1) * 128, :], dt_[:], r=['dbgt'], w=[('dbgo', name, c)], skey='dbgt')
    P.barrier()
    st.close()
    P.in_shapes = in_shapes
    return nc, P


def _mult_mask():
    kk = np.arange(128)[:, None, None]
    kt = np.arange(17)[None, :, None] - 8
    qq = np.arange(128)[None, None, :]
    d = 128 * kt + kk - qq
    ad = np.abs(d)
    c = (ad <= 64).astype(np.float32) + ((d % 4 == 0) & (ad <= 256)).astype(np.float32) \
        + ((d % 16 == 0) & (ad <= 1024)).astype(np.float32)
    return np.ascontiguousarray(c.astype(np.float32))


def _rope_table(pos, nheads, half, theta=500000.0):
    inv = theta ** (-np.arange(half, dtype=np.float32) / half)
    ang = pos.astype(np.float32)[:, None] * inv[None, :]
    cos = np.cos(ang).astype(np.float32)
    sin = np.sin(ang).astype(np.float32)
    out = np.stack([np.tile(cos, (1, nheads)), np.tile(sin, (1, nheads))], axis=1)
    return np.ascontiguousarray(out.astype(np.float32))


def make_inputs(core, x_prompt, x_sample, mem_prompt, mem_sample, norm_gains, w_in, w_out, b_w_spatial, b_b_spatial,
                b_ln_gain, c_lambda, c_subln_gain, d_w_fourier, mem_norm_gain, ca_w_q, ca_w_kv, ca_w_o, ffn_w_up,
                ffn_conv_w, ffn_conv_b, ffn_w_down, shared):
    c = core
    f32 = np.float32
    segsrc = [x_prompt[0], x_prompt[1], x_sample[0]]
    x_in = np.concatenate([segsrc[s][c * SEG_LEN[s]:(c + 1) * SEG_LEN[s]] for s in range(3)], axis=0)
    halos = []
    pos_h = []
    valid = np.zeros(ETOT, f32)
    for s in range(3):
        L = SEG_LEN[s]
        S = SEG_S[s]
        a = c * L
        for (lo, hi) in ((a - 1024, a), (a + L, a + L + 1024)):
            blk = np.zeros((1024, D), f32)
            l2, h2 = max(lo, 0), min(hi, S)
            if h2 > l2:
                blk[l2 - lo:h2 - lo] = segsrc[s][l2:h2]
            halos.append(blk)
            pos_h.append(np.arange(lo, hi))
        epos = np.arange(a - 1024, a + L + 1024)
        valid[EOFF[s]:EOFF[s] + L + 2048] = ((epos >= 0) & (epos < S)).astype(f32)
    x_halo = np.concatenate(halos, axis=0)
    pos_main = np.concatenate([np.arange(c * SEG_LEN[s], (c + 1) * SEG_LEN[s]) for s in range(3)])
    pos_all = np.concatenate([pos_main] + pos_h)
    ropeA = _rope_table(pos_all, 8, 16)
    ropeC = _rope_table(pos_main, 24, 8)
    validT = np.ascontiguousarray(valid.reshape(72, 128).T)
    sel = np.zeros((54, 12), f32)
    def setsel(col, rank, i):
        if 0 <= rank < NCORES:
            sel[rank * 6 + i, col] = 1.0
    setsel(0, c - 1, 1); setsel(1, c + 1, 0)
    setsel(2, c - 1, 3); setsel(3, c + 1, 2)
    setsel(4, c - 1, 5); sel[48 + 1, 5] = 1.0
    sel[48 + 0, 6] = 1.0; sel[48 + 3, 7] = 1.0
    sel[48 + 2, 8] = 1.0; sel[48 + 5, 9] = 1.0
    sel[48 + 4, 10] = 1.0; setsel(11, c + 1, 4)
    k1S = np.arange(16 * c, 16 * c + 16)
    s1 = np.arange(128)
    angS = 2 * np.pi * ((s1[:, None] * k1S[None, :]) % 128) / 128.0
    cwS = np.stack([np.cos(angS), -np.sin(angS), -np.cos(angS)], axis=1).astype(f32)
    k1P = np.arange(4 * c, 4 * c + 4)
    s1p = np.arange(32)
    angP = 2 * np.pi * ((s1p[:, None] * k1P[None, :]) % 32) / 32.0
    cwP = np.stack([np.cos(angP), -np.sin(angP), -np.cos(angP)], axis=1).astype(f32)
    m = dict(shared)
    m.update({
        "x_in": np.ascontiguousarray(x_in), "x_halo": x_halo, "ropeA": ropeA, "ropeC": ropeC, "validT": validT,
        "sel": sel, "cwS": np.ascontiguousarray(cwS), "cwP": np.ascontiguousarray(cwP),
    })
    return m


def make_shared(x_prompt, x_sample, mem_prompt, mem_sample, norm_gains, w_in, w_out, b_w_spatial, b_b_spatial,
                b_ln_gain, c_lambda, c_subln_gain, d_w_fourier, mem_norm_gain, ca_w_q, ca_w_kv, ca_w_o, ffn_w_up,
                ffn_conv_w, ffn_conv_b, ffn_w_down):
    f32 = np.float32
    A = np.ascontiguousarray
    mem = np.concatenate([mem_prompt[0], mem_prompt[1], mem_sample[0]], axis=0)
    gT = A(norm_gains.reshape(2, 6, KC, 128).transpose(3, 0, 1, 2).reshape(128, 2 * 6 * KC))
    gmT = A(mem_norm_gain.reshape(2, KC, 128).transpose(2, 0, 1).reshape(128, 2 * KC))
    wsT = A(b_w_spatial[0].transpose(2, 0, 1))
    bsT = A(b_b_spatial[0].T)
    cwT = np.zeros((128, 2, 88, 4), f32)
    cwT[:, :, :, 0:3] = ffn_conv_w.reshape(2, 3, 88, 128).transpose(3, 0, 2, 1)
    cwT[:, :, :, 3] = ffn_conv_b.reshape(2, 88, 128).transpose(2, 0, 1)
    j = np.arange(128)
    ang = 2 * np.pi * ((j[:, None] * j[None, :]) % 128) / 128.0
    dft128 = np.stack([np.cos(ang), np.sin(ang)], axis=1).astype(f32)
    j32 = np.arange(32)
    ang32 = 2 * np.pi * ((j32[:, None] * j32[None, :]) % 32) / 32.0
    dft32 = np.stack([np.cos(ang32), np.sin(ang32)], axis=1).astype(f32)

    def tw(N1, S):
        s1 = np.arange(N1)[:, None].astype(np.float64)
        k2 = np.arange(128)[None, :].astype(np.float64)
        a = 2 * np.pi * s1 * k2 / S
        sc = 1.0 / math.sqrt(S * 128.0)
        t = np.stack([np.tile(np.cos(a) * sc, (1, 4)), np.tile(np.sin(a) * sc, (1, 4))], axis=1)
        return A(t.astype(f32))
    return {
        "mem": A(mem), "gT": gT, "gmT": gmT, "norm_gains": A(norm_gains), "w_in": A(w_in), "w_out": A(w_out),
        "wsT": wsT, "bsT": bsT, "gln": A(b_ln_gain[0].reshape(1, 1024)), "c_lambda": A(c_lambda[0].reshape(1, 256)),
        "c_subln": A(c_subln_gain[0].reshape(1, 128)), "d_w_fourier": A(d_w_fourier[0]), "ca_w_q": A(ca_w_q),
        "ca_w_kv": A(ca_w_kv), "ca_w_o": A(ca_w_o), "ffn_w_up": A(ffn_w_up), "cwT": cwT, "ffn_w_down": A(ffn_w_down),
        "ident": np.eye(128, dtype=f32), "cmask": _mult_mask(), "dft128": A(dft128), "dft32": A(dft32),
        "twS": tw(128, 16384), "twP": tw(32, 4096),
    }


def run(inputs, dbg=None):
    inputs = {k: np.asarray(v, dtype=np.float32) for k, v in inputs.items()}
    nc, P = build_program(dbg)
    shared = make_shared(**inputs)
    in_maps = [make_inputs(c, shared=shared, **inputs) for c in range(NCORES)]
    for m in in_maps:
        for k in list(m.keys()):
            if list(m[k].shape) != P.in_shapes[k]:
                m[k] = np.zeros(P.in_shapes[k], np.float32)
    res = run_bass_kernel_spmd(nc, in_maps, core_ids=list(range(NCORES)))
    return res


def assemble(per_core):
    yp = np.zeros((2, 4096, D), np.float32)
    ys = np.zeros((1, 16384, D), np.float32)
    for c in range(NCORES):
        y = per_core[c]
        yp[0, c * 512:(c + 1) * 512] = y[0:512]
        yp[1, c * 512:(c + 1) * 512] = y[512:1024]
        ys[0, c * 2048:(c + 1) * 2048] = y[1024:3072]
    return yp, ys


def kernel(**inputs):
    res = run(inputs)
    return assemble([res.results[c]["y"] for c in range(NCORES)])
```
